# Optimizing a Trainium2 kernel written in Bass

```python
import jax, jax.numpy as jnp
from jax import lax
import numpy as np

D_MODEL = 2048
BATCH = 4
SEQ = 4096
DEPTH = 4

HEAD_DIM = 128
ROPE_THETA = 10000.0
BLOCK_Q = 128
EPS = 1e-6
NEG_INF = -1e30
A_HEADS = 8
A_KV_HEADS = 2
IDX_HEADS = 16
IDX_DIM = 128
IDX_TOPK = 256
CONV_CH = 1024
CONV_WIDTH = 31
C_HEADS = 8
N_BRANCH = 3

A_WIDTH = A_HEADS * HEAD_DIM
A_KV_WIDTH = A_KV_HEADS * HEAD_DIM
C_WIDTH = C_HEADS * HEAD_DIM
SPLITS = (A_WIDTH, A_KV_WIDTH, A_KV_WIDTH, A_WIDTH,
          IDX_HEADS * IDX_DIM, IDX_DIM, IDX_HEADS,
          2 * CONV_CH, CONV_CH,
          C_WIDTH, C_WIDTH, C_WIDTH, C_WIDTH,
          N_BRANCH * D_MODEL)
IN_WIDTH = sum(SPLITS)

kernel_name = "hybrid_dsa_conformer_stickbreak_block"


def rms_norm(x, w):
    xf = x.astype(jnp.float32)
    y = xf * lax.rsqrt(jnp.mean(xf * xf, axis=-1, keepdims=True) + EPS)
    return (y * w.astype(jnp.float32)).astype(x.dtype)


def layer_norm(x, w, b):
    xf = x.astype(jnp.float32)
    mu = jnp.mean(xf, axis=-1, keepdims=True)
    var = jnp.mean(jnp.square(xf - mu), axis=-1, keepdims=True)
    y = (xf - mu) * lax.rsqrt(var + EPS)
    return (y * w.astype(jnp.float32) + b.astype(jnp.float32)).astype(x.dtype)


def rope_tables(seq, dim):
    inv = 1.0 / (ROPE_THETA ** (jnp.arange(0, dim, 2, dtype=jnp.float32) / dim))
    ang = jnp.arange(seq, dtype=jnp.float32)[:, None] * inv[None, :]
    return jnp.cos(ang), jnp.sin(ang)


def apply_rope(x, cos, sin):
    half = x.shape[-1] // 2
    xf = x.astype(jnp.float32)
    x1, x2 = xf[..., :half], xf[..., half:]
    c, s = cos[None, :, None, :], sin[None, :, None, :]
    return jnp.concatenate([x1 * c - x2 * s, x2 * c + x1 * s], axis=-1).astype(x.dtype)


def split_points():
    return [int(o) for o in np.cumsum(np.array(SPLITS))[:-1]]


def dsa_attention(q, k, v, qi, ki, wi, top_k):
    b, s, h, dh = q.shape
    g = h // A_KV_HEADS
    gather = jax.vmap(lambda arr, ids: arr[ids])
    outs = []
    for start in range(0, s, BLOCK_Q):
        end = start + BLOCK_Q
        tq = start + jnp.arange(BLOCK_Q)
        causal = jnp.arange(end)[None, :] <= tq[:, None]
        rel = jnp.einsum('bthd,bsd->bhts', qi[:, start:end].astype(jnp.float32), ki[:, :end].astype(jnp.float32))
        score = jnp.einsum('bhts,bth->bts', jax.nn.relu(rel), wi[:, start:end].astype(jnp.float32))
        score = jnp.where(causal[None], score, NEG_INF)
        kk = min(top_k, end)
        _, idx = lax.top_k(score, kk)
        valid = idx <= tq[None, :, None]
        ks = gather(k, idx)
        vs = gather(v, idx)
        qb = q[:, start:end].reshape(b, BLOCK_Q, A_KV_HEADS, g, dh)
        logits = jnp.einsum('btcgd,btncd->btcgn', qb.astype(jnp.float32), ks.astype(jnp.float32)) * (dh ** -0.5)
        logits = jnp.where(valid[:, :, None, None, :], logits, NEG_INF)
        p = jax.nn.softmax(logits, axis=-1)
        o = jnp.einsum('btcgn,btncd->btcgd', p.astype(v.dtype), vs)
        outs.append(o.reshape(b, BLOCK_Q, h * dh))
    return jnp.concatenate(outs, axis=1)


def stick_breaking_attention(q, k, v):
    b, s, h, dh = q.shape
    outs = []
    for start in range(0, s, BLOCK_Q):
        end = start + BLOCK_Q
        tq = start + jnp.arange(BLOCK_Q)
        strict = (jnp.arange(end)[None, :] < tq[:, None])[None, None]
        z = jnp.einsum('bthd,bshd->bhts', q[:, start:end].astype(jnp.float32), k[:, :end].astype(jnp.float32)) * (dh ** -0.5)
        log_beta = jax.nn.log_sigmoid(z)
        log_keep = jnp.where(strict, jax.nn.log_sigmoid(-z), 0.0)
        after = lax.cumsum(log_keep, axis=3, reverse=True) - log_keep
        a = jnp.where(strict, jnp.exp(log_beta + after), 0.0)
        o = jnp.einsum('bhts,bshd->bthd', a.astype(v.dtype), v[:, :end])
        outs.append(o.reshape(b, BLOCK_Q, h * dh))
    return jnp.concatenate(outs, axis=1)


def conformer_conv(u, conv_w, conv_b, ln_w, ln_b):
    a, gt = jnp.split(u, 2, axis=-1)
    y = a * jax.nn.sigmoid(gt)
    y = lax.conv_general_dilated(y, conv_w.astype(y.dtype), window_strides=(1,),
                                 padding=[(CONV_WIDTH - 1, 0)],
                                 dimension_numbers=('NWC', 'WIO', 'NWC'),
                                 feature_group_count=CONV_CH) + conv_b
    y = layer_norm(y, ln_w, ln_b)
    return jax.nn.silu(y)


def hybrid_layer(x, norm_w, w_in, conv_w, conv_b, cln_w, cln_b, w_a_out, w_b_out, w_c_out, w_out, cos, sin, top_k):
    b, s, _ = x.shape
    h = rms_norm(x, norm_w)
    proj = h @ w_in
    (qa, ka, va, za, qi, ki, wi, ub, zb, qc, kc, vc, zc, gates) = jnp.split(proj, split_points(), axis=-1)
    qa = apply_rope(qa.reshape(b, s, A_HEADS, HEAD_DIM), cos, sin)
    ka = apply_rope(ka.reshape(b, s, A_KV_HEADS, HEAD_DIM), cos, sin)
    va = va.reshape(b, s, A_KV_HEADS, HEAD_DIM)
    qi = apply_rope(qi.reshape(b, s, IDX_HEADS, IDX_DIM), cos, sin)
    ki = apply_rope(ki.reshape(b, s, 1, IDX_DIM), cos, sin)[:, :, 0]
    wi = wi * (IDX_HEADS ** -0.5) * (IDX_DIM ** -0.5)
    y_a = (dsa_attention(qa, ka, va, qi, ki, wi, top_k) * jax.nn.silu(za)) @ w_a_out
    y_b = (conformer_conv(ub, conv_w, conv_b, cln_w, cln_b) * jax.nn.silu(zb)) @ w_b_out
    qc = qc.reshape(b, s, C_HEADS, HEAD_DIM)
    kc = kc.reshape(b, s, C_HEADS, HEAD_DIM)
    vc = vc.reshape(b, s, C_HEADS, HEAD_DIM)
    y_c = (stick_breaking_attention(qc, kc, vc) * jax.nn.silu(zc)) @ w_c_out
    g = jax.nn.sigmoid(gates).reshape(b, s, N_BRANCH, D_MODEL)
    merged = g[:, :, 0] * y_a + g[:, :, 1] * y_b + g[:, :, 2] * y_c
    return x + merged @ w_out


def setup_inputs(seed: int = 0) -> dict:
    key = jax.random.key(seed)
    ks = jax.random.split(key, 14)
    f32 = jnp.float32
    nrm = lambda k, shape, scale: jax.random.normal(k, shape, f32) * scale
    return {
        "x": nrm(ks[0], (BATCH, SEQ, D_MODEL), 1.0),
        "norm_w": 1.0 + nrm(ks[1], (DEPTH, D_MODEL), 0.02),
        "w_in": nrm(ks[2], (DEPTH, D_MODEL, IN_WIDTH), D_MODEL ** -0.5),
        "conv_w": nrm(ks[3], (DEPTH, CONV_WIDTH, 1, CONV_CH), CONV_WIDTH ** -0.5),
        "conv_b": nrm(ks[4], (DEPTH, CONV_CH), 0.01),
        "conv_ln_w": 1.0 + nrm(ks[5], (DEPTH, CONV_CH), 0.02),
        "conv_ln_b": nrm(ks[6], (DEPTH, CONV_CH), 0.01),
        "w_a_out": nrm(ks[7], (DEPTH, A_WIDTH, D_MODEL), A_WIDTH ** -0.5),
        "w_b_out": nrm(ks[8], (DEPTH, CONV_CH, D_MODEL), CONV_CH ** -0.5),
        "w_c_out": nrm(ks[9], (DEPTH, C_WIDTH, D_MODEL), C_WIDTH ** -0.5),
        "w_out": nrm(ks[10], (DEPTH, D_MODEL, D_MODEL), (2.0 * D_MODEL) ** -0.5),
        "final_norm_w": 1.0 + nrm(ks[11], (D_MODEL,), 0.02),
    }


def reference(x, norm_w, w_in, conv_w, conv_b, conv_ln_w, conv_ln_b, w_a_out, w_b_out, w_c_out, w_out, final_norm_w):
    seq = x.shape[1]
    top_k = min(IDX_TOPK, seq // 4)
    cos, sin = rope_tables(seq, HEAD_DIM)
    for i in range(DEPTH):
        x = hybrid_layer(x, norm_w[i], w_in[i], conv_w[i], conv_b[i], conv_ln_w[i], conv_ln_b[i],
                         w_a_out[i], w_b_out[i], w_c_out[i], w_out[i], cos, sin, top_k)
    return rms_norm(x, final_norm_w)
```

```python
from contextlib import ExitStack
import numpy as np
import ml_dtypes
import concourse.bass as bass
import concourse.mybir as mybir
from concourse.bass_utils import run_bass_kernel_spmd

F32 = mybir.dt.float32
BF16 = mybir.dt.bfloat16
AF = mybir.ActivationFunctionType
ALU = mybir.AluOpType
AX = mybir.AxisListType

D = 2048
HD = 128
IN_WIDTH = 18064
EPS = 1e-6
NEG = -1.0e30
TOPK = 256
CW = 31
O_QA, O_KA, O_VA, O_ZA = 0, 1024, 1280, 1536
O_QI, O_KI, O_WI = 2560, 4608, 4736
O_UA, O_UG, O_ZB = 4752, 5776, 6800
O_QC, O_KC, O_VC, O_ZC, O_G = 7824, 8848, 9872, 10896, 11920
FM_ZA, FM_ZB, FM_QC, FM_KC, FM_ZC, FM_G, NFM = 0, 8, 16, 24, 32, 40, 88

EPOCH = 10 ** 9


class Prog:
    CE = ("pe", "act", "dve", "pool")

    def __init__(self, nc, n_lanes=16):
        self.nc = nc
        self.eng = {"pe": nc.tensor, "act": nc.scalar, "dve": nc.vector,
                    "pool": nc.gpsimd, "sp": nc.sync}
        self.ops = {e: [] for e in self.eng}
        self.known = {e: {} for e in self.eng}
        self.nsem = 0
        self.cur_sem = {}
        self.cur_cnt = {}
        for e in self.CE:
            self._new_epoch(e)
        self.lanes = [[nc.alloc_semaphore(f"lane{i}"), 0] for i in range(n_lanes)]
        self.lane_rr = 0
        self.lastw = {}
        self.readers = {}
        self.n_inst = 0

    def _new_epoch(self, e):
        s = self.nc.alloc_semaphore(f"sem_{e}_{self.nsem}")
        self.nsem += 1
        self.cur_sem[e] = s
        self.cur_cnt[e] = 0

    def _deps(self, reads, writes):
        toks = []
        for k in reads:
            t = self.lastw.get(k)
            if t is not None:
                toks.append(t)
        for k in writes:
            t = self.lastw.get(k)
            if t is not None:
                toks.append(t)
            toks.extend(self.readers.get(k, ()))
        return toks

    def _commit(self, tok, reads, writes):
        for k in reads:
            self.readers.setdefault(k, []).append(tok)
        for k in writes:
            self.lastw[k] = tok
            self.readers[k] = []

    def _waits(self, e, toks, skip_sem=None):
        kn = self.known[e]
        best = {}
        for (s, v) in toks:
            if skip_sem is not None and s is skip_sem:
                continue
            key = id(s)
            if kn.get(key, 0) >= v:
                continue
            if key not in best or best[key][1] < v:
                best[key] = (s, v)
        out = []
        for key, (s, v) in best.items():
            kn[key] = v
            out.append((s, v))
        return out

    def op(self, e, fn, reads=(), writes=(), extra=()):
        toks = self._deps(reads, writes) + list(extra)
        if self.cur_cnt[e] >= EPOCH:
            self._new_epoch(e)
        skip = self.cur_sem[e] if e == "pe" else None
        waits = self._waits(e, toks, skip_sem=skip)
        self.cur_cnt[e] += 1
        tok = (self.cur_sem[e], self.cur_cnt[e])
        self.ops[e].append((waits, fn, tok[0], 1))
        self._commit(tok, reads, writes)
        self.n_inst += 1
        return tok

    def dma(self, out, in_, reads=(), writes=(), q="sp", extra=(), **kw):
        toks = self._deps(reads, writes) + list(extra)
        lane = self.lanes[self.lane_rr]
        self.lane_rr = (self.lane_rr + 1) % len(self.lanes)
        if lane[1] > 0:
            toks.append((lane[0], 16 * lane[1]))
        waits = self._waits(q, toks)
        lane[1] += 1
        tok = (lane[0], 16 * lane[1])
        self.ops[q].append((waits, lambda eng: eng.dma_start(out=out, in_=in_, **kw), tok[0], 16))
        self._commit(tok, reads, writes)
        self.n_inst += 1
        return tok

    def end_phase(self):
        toks = [(l[0], 16 * l[1]) for l in self.lanes if l[1] > 0]
        waits = self._waits("sp", toks)
        self.ops["sp"].append((waits, None, None, 0))
        nc = self.nc
        with nc.Block() as block:
            def mk(e):
                def body(eng):
                    for waits, fn, sem, inc in self.ops[e]:
                        for (s, v) in waits:
                            eng.wait_ge(s, v)
                        if fn is not None:
                            fn(eng).then_inc(sem, inc)
                return body
            block.tensor(mk("pe"))
            block.scalar(mk("act"))
            block.vector(mk("dve"))
            block.gpsimd(mk("pool"))
            block.sync(mk("sp"))
        self.ops = {e: [] for e in self.eng}
        self.lastw = {}
        self.readers = {}

    def loop_top(self):
        nc = self.nc
        sems = [self.cur_sem[e] for e in self.CE] + [l[0] for l in self.lanes]
        with nc.Block() as block:
            def body(eng):
                for sm in sems:
                    eng.sem_clear(sm)
            block.gpsimd(body)
        for e in self.CE:
            self.cur_cnt[e] = 0
        for l in self.lanes:
            l[1] = 0
        self.known = {e: {} for e in self.eng}
        self.lastw = {}
        self.readers = {}


_UID = [0]


def _uniq(name):
    _UID[0] += 1
    return f"{name}_u{_UID[0]}"


class Ring:
    def __init__(self, st, nc, name, shape, dtype, n):
        self.bufs = []
        for i in range(n):
            t = st.enter_context(nc.sbuf_tensor(_uniq(f"{name}{i}"), shape, dtype))
            self.bufs.append((t.ap() if hasattr(t, "ap") else t, f"{name}{i}"))
        self.i = 0

    def next(self):
        b = self.bufs[self.i]
        self.i = (self.i + 1) % len(self.bufs)
        return b


class PRing:
    def __init__(self, banks, idx):
        self.bufs = [(banks[i], f"psum{i}") for i in idx]
        self.i = 0

    def next(self):
        b = self.bufs[self.i]
        self.i = (self.i + 1) % len(self.bufs)
        return b


def sb(st, nc, name, shape, dtype):
    t = st.enter_context(nc.sbuf_tensor(_uniq(name), shape, dtype))
    return t.ap() if hasattr(t, "ap") else t


def build(S, L, dbg=False):
    nc = bass.Bass("TRN2", target_bir_lowering=False)
    NT = S // 128
    NQC = S // 512

    def din(name, shape, dt=F32):
        return nc.dram_tensor(name, shape, dt, kind="ExternalInput").ap()

    def dscr(name, shape, dt):
        kind = "ExternalOutput" if dbg else "Internal"
        return nc.dram_tensor(name, shape, dt, kind=kind).ap()

    x_in = din("x", [S, D])
    norm_w = din("norm_w", [L, D])
    w_in = din("w_in", [L, D, IN_WIDTH])
    conv_w = din("conv_w", [L, 128, 8, CW])
    conv_b = din("conv_b", [L, 128, 8])
    cln_w = din("conv_ln_w", [L, 128, 8])
    cln_b = din("conv_ln_b", [L, 128, 8])
    w_ao = din("w_a_out", [L, 1024, D])
    w_bo = din("w_b_out", [L, 1024, D])
    w_co = din("w_c_out", [L, 1024, D])
    w_o = din("w_out", [L, D, D])
    fnw = din("final_norm_w", [D])
    rope = din("rope", [S, 192])
    out = nc.dram_tensor("out", [S, D], F32, kind="ExternalOutput").ap()

    XS = [dscr("xs0", [S, D], F32), dscr("xs1", [S, D], F32)]
    FMS = dscr("fms", [NFM, 128, S], BF16)
    YT = dscr("yt", [8, 128, S], F32)
    QAT = dscr("qat", [8, 128, S], BF16)
    KAT = dscr("kat", [2, 128, S], BF16)
    QIT = dscr("qit", [16, 128, S], BF16)
    KIT = dscr("kit", [128, S], BF16)
    VA = dscr("va", [S, 256], BF16)
    VC = dscr("vc", [S, 1024], BF16)
    WI = dscr("wi", [S, 16], F32)
    OGT = dscr("ogt", [24, 128, S], BF16)
    MGT = dscr("mgt", [16, 128, S], BF16)

    P = Prog(nc)
    gst = ExitStack()
    banks = []
    for i in range(8):
        t = gst.enter_context(nc.psum_tensor(f"bank{i}", [128, 512], F32))
        banks.append(t.ap() if hasattr(t, "ap") else t)

    identb = sb(gst, nc, "identb", [128, 128], BF16)
    onesb = sb(gst, nc, "onesb", [128, 128], BF16)
    onesf = sb(gst, nc, "onesf", [128, 128], F32)
    ones512 = sb(gst, nc, "ones512", [128, 512], BF16)
    zer512b = sb(gst, nc, "zer512b", [128, 512], BF16)
    zer512f = sb(gst, nc, "zer512f", [128, 512], F32)
    utri = sb(gst, nc, "utri", [128, 128], BF16)

    P.op("pool", lambda e: e.memset(onesb, 1.0), writes=["c_onesb"])
    P.op("pool", lambda e: e.memset(onesf, 1.0), writes=["c_onesf"])
    P.op("pool", lambda e: e.memset(ones512, 1.0), writes=["c_ones512"])
    P.op("pool", lambda e: e.memset(zer512b, 0.0), writes=["c_zer512b"])
    P.op("pool", lambda e: e.memset(zer512f, 0.0), writes=["c_zer512f"])
    P.op("pool", lambda e: e.affine_select(out=identb, in_=zer512b[:, 0:128], pattern=[[-1, 128]],
                                           compare_op=ALU.not_equal, fill=1.0, base=0, channel_multiplier=1),
         reads=["c_zer512b"], writes=["c_ident"])
    P.op("pool", lambda e: e.affine_select(out=utri, in_=onesb, pattern=[[-1, 128]],
                                           compare_op=ALU.is_ge, fill=0.0, base=-1, channel_multiplier=1),
         reads=["c_onesb"], writes=["c_utri"])
    for r0 in range(0, S, 512):
        P.dma(XS[0][r0:r0 + 512, :], x_in[r0:r0 + 512, :], writes=["xs0"])
    P.end_phase()

    def load_weight_chunk(l, wsrc, pieces, wst, wbf, nk):
        wb, wbk = wbf.next()
        for kh in range(nk // 8):
            ws, wsk = wst.next()
            coff = 0
            for (c0, n) in pieces:
                src = wsrc[l, kh * 1024:(kh + 1) * 1024, c0:c0 + n].rearrange("(k p) n -> p k n", p=128)
                P.dma(ws[:, :, coff:coff + n], src, writes=[wsk])
                coff += n
            P.op("pool", lambda e, ws=ws, wb=wb, kh=kh, coff=coff:
                 e.tensor_copy(out=wb[:, kh * 8:(kh + 1) * 8, 0:coff], in_=ws[:, :, 0:coff]),
                 reads=[wsk], writes=[wbk])
        return wb, wbk

    def rmsnorm_tile(xt, xk, nwt, nwk, hb, hbk, junk, jk, ss, ssk):
        P.op("act", lambda e: e.activation(out=junk, in_=xt, func=AF.Square, accum_out=ss),
             reads=[xk], writes=[jk, ssk])
        P.op("dve", lambda e: e.tensor_scalar(out=ss, in0=ss, scalar1=1.0 / D, scalar2=EPS,
                                              op0=ALU.mult, op1=ALU.add), reads=[ssk], writes=[ssk])
        P.op("act", lambda e: e.activation(out=ss, in_=ss, func=AF.Sqrt), reads=[ssk], writes=[ssk])
        P.op("dve", lambda e: e.reciprocal(out=ss, in_=ss), reads=[ssk], writes=[ssk])
        P.op("dve", lambda e: e.scalar_tensor_tensor(out=hb, in0=xt, scalar=ss, in1=nwt,
                                                     op0=ALU.mult, op1=ALU.mult),
             reads=[xk, ssk, nwk], writes=[hbk])

    def phase_A(l, xcur, xkey):
        T = min(S, 1024)
        NG = S // T
        NTT = T // 128
        with ExitStack() as st:
            hT = sb(st, nc, "hT", [128, 16, T], BF16)
            nwt = sb(st, nc, "nwt", [128, D], F32)
            tab = sb(st, nc, "tab", [128, NTT, 192], F32)
            xr = Ring(st, nc, "xt", [128, D], F32, 2)
            hbr = Ring(st, nc, "hb", [128, D], BF16, 2)
            ssr = Ring(st, nc, "ss", [128, 1], F32, 2)
            wst = Ring(st, nc, "wst", [128, 8, 512], F32, 2)
            wbf = Ring(st, nc, "wbf", [128, 16, 512], BF16, 2)
            rar = Ring(st, nc, "ra", [128, 512], F32, 2)
            rbr = Ring(st, nc, "rb", [128, 512], F32, 2)
            rrr = Ring(st, nc, "rr", [128, 512], BF16, 2)
            tms = Ring(st, nc, "tms", [128, 4, T], BF16, 2)
            fst = Ring(st, nc, "fst", [128, 512], BF16, 4)
            f32r = Ring(st, nc, "f32r", [128, 512], F32, 3)
            wir = Ring(st, nc, "wir", [128, 16], F32, 2)
            pmm = PRing(banks, [0, 1, 2, 3])
            ptr = PRing(banks, [4, 5, 6, 7])

            def transposes_to(src_bf, srck, nheads, dst_fn):
                h = 0
                while h < nheads:
                    nb = min(4, nheads - h)
                    pb, pk = ptr.next()
                    pbb = pb.bitcast(BF16)
                    for j in range(nb):
                        P.op("pe", lambda e, j=j, h=h, pbb=pbb: e.transpose(
                            out=pbb[:, j * 128:(j + 1) * 128], in_=src_bf[:, (h + j) * 128:(h + j + 1) * 128],
                            identity=identb), reads=[srck, "c_ident"], writes=[pk])
                    for j in range(nb):
                        dst, dk = dst_fn(h + j)
                        P.op("act", lambda e, j=j, pbb=pbb, dst=dst: e.activation(
                            out=dst, in_=pbb[:, j * 128:(j + 1) * 128], func=AF.Copy),
                            reads=[pk], writes=[dk])
                    h += nb

            def rope_chunk(ps, pk, nh, tt):
                ra, rak = rar.next()
                rb, rbk = rbr.next()
                rr, rrk = rrr.next()
                n = nh * 128
                v4 = lambda a: a[:, 0:n].rearrange("p (h t d) -> p h t d", h=nh, t=2)
                cosb = tab[:, tt, 0:64].unsqueeze(1).unsqueeze(1).to_broadcast([128, nh, 2, 64])
                sinb = tab[:, tt, 64:128].unsqueeze(1).to_broadcast([128, nh, 64])
                nsinb = tab[:, tt, 128:192].unsqueeze(1).to_broadcast([128, nh, 64])
                P.op("dve", lambda e: e.tensor_tensor(out=v4(ra), in0=v4(ps), in1=cosb, op=ALU.mult),
                     reads=[pk, "tab"], writes=[rak])
                P.op("dve", lambda e: e.tensor_tensor(out=v4(rb)[:, :, 0, :], in0=v4(ps)[:, :, 1, :], in1=nsinb,
                                                      op=ALU.mult), reads=[pk, "tab"], writes=[rbk])
                P.op("dve", lambda e: e.tensor_tensor(out=v4(rb)[:, :, 1, :], in0=v4(ps)[:, :, 0, :], in1=sinb,
                                                      op=ALU.mult), reads=[pk, "tab"], writes=[rbk])
                P.op("dve", lambda e: e.tensor_tensor(out=rr[:, 0:n], in0=ra[:, 0:n], in1=rb[:, 0:n], op=ALU.add),
                     reads=[rak, rbk], writes=[rrk])
                return rr, rrk

            chunks = []
            chunks.append(("tm", [(O_QA, 512)], ("rope", QAT, 0, "QAT")))
            chunks.append(("tm", [(O_QA + 512, 512)], ("rope", QAT, 4, "QAT")))
            chunks.append(("tm", [(O_KA, 512)], ("kv",)))
            for i in range(4):
                chunks.append(("tm", [(O_QI + 512 * i, 512)], ("rope", QIT, 4 * i, "QIT")))
            chunks.append(("tm", [(O_KI, 256)], ("kiwi",)))
            chunks.append(("tm", [(O_VC, 512)], ("vc", 0)))
            chunks.append(("tm", [(O_VC + 512, 512)], ("vc", 1)))
            for i in range(2):
                chunks.append(("fm", [(O_ZA + 512 * i, 512)], ("silu", FM_ZA + 4 * i)))
            for i in range(4):
                chunks.append(("fm", [(O_UA + 256 * i, 256), (O_UG + 256 * i, 256)], ("glu", 2 * i)))
            for i in range(2):
                chunks.append(("fm", [(O_ZB + 512 * i, 512)], ("silu", FM_ZB + 4 * i)))
            for i in range(2):
                chunks.append(("fm", [(O_QC + 512 * i, 512)], ("copy", FM_QC + 4 * i)))
            for i in range(2):
                chunks.append(("fm", [(O_KC + 512 * i, 512)], ("copy", FM_KC + 4 * i)))
            for i in range(2):
                chunks.append(("fm", [(O_ZC + 512 * i, 512)], ("silu", FM_ZC + 4 * i)))
            for i in range(12):
                chunks.append(("fm", [(O_G + 512 * i, 512)], ("sigm", FM_G + 4 * i)))

            for g in range(NG):
                g0 = g * T
                P.dma(nwt, norm_w[l].partition_broadcast(128), writes=["nwt"])
                P.dma(tab, rope[bass.ds(g0, T), :].rearrange("(t p) c -> p t c", p=128), writes=["tab"])
                for tt in range(NTT):
                    xt, xk = xr.next()
                    hb, hbk = hbr.next()
                    ss, ssk = ssr.next()
                    P.dma(xt, xcur[bass.ds(g0 + tt * 128, 128), :], reads=[xkey], writes=[xk])
                    rmsnorm_tile(xt, xk, nwt, "nwt", hb, hbk, hb, hbk, ss, ssk)
                    for half in range(2):
                        pb, pk = ptr.next()
                        pbb = pb.bitcast(BF16)
                        for j in range(8):
                            k = half * 8 + j
                            P.op("pe", lambda e, j=j, k=k, pbb=pbb, hb=hb: e.transpose(
                                out=pbb[:, j * 128:(j + 1) * 128], in_=hb[:, k * 128:(k + 1) * 128],
                                identity=identb), reads=[hbk, "c_ident"], writes=[pk])
                        P.op("dve" if half else "act",
                             (lambda e, pbb=pbb, half=half, tt=tt: e.tensor_copy(
                                 out=hT[:, half * 8:(half + 1) * 8, tt * 128:(tt + 1) * 128],
                                 in_=pbb.rearrange("p (k t) -> p k t", k=8))) if half else
                             (lambda e, pbb=pbb, half=half, tt=tt: e.activation(
                                 out=hT[:, half * 8:(half + 1) * 8, tt * 128:(tt + 1) * 128],
                                 in_=pbb.rearrange("p (k t) -> p k t", k=8), func=AF.Copy)),
                             reads=[pk], writes=["hT"])
                nxt = load_weight_chunk(l, w_in, chunks[0][1], wst, wbf, 16)
                for ci, (mode, pieces, h) in enumerate(chunks):
                    ncols = sum(n for _, n in pieces)
                    if h[0] == "kiwi":
                        ncols = 144
                    wb, wbk = nxt
                    if ci + 1 < len(chunks):
                        nxt = load_weight_chunk(l, w_in, chunks[ci + 1][1], wst, wbf, 16)
                    if mode == "tm":
                        stg = None
                        if h[0] in ("rope",):
                            stg, stk = tms.next()
                        if h[0] == "kv":
                            stg, stk = tms.next()
                        if h[0] == "kiwi":
                            stg, stk = tms.next()
                        for tt in range(NTT):
                            t0 = g0 + tt * 128
                            ps, pk = pmm.next()
                            for k in range(16):
                                P.op("pe", lambda e, k=k, ps=ps, tt=tt, wb=wb, ncols=ncols: e.matmul(
                                    ps[:, 0:ncols], lhsT=hT[:, k, tt * 128:(tt + 1) * 128], rhs=wb[:, k, 0:ncols],
                                    start=(k == 0), stop=(k == 15)), reads=["hT", wbk], writes=[pk])
                            if h[0] == "rope":
                                rr, rrk = rope_chunk(ps, pk, 4, tt)
                                transposes_to(rr, rrk, 4, lambda hh, tt=tt, stg=stg, stk=stk:
                                              (stg[:, hh, tt * 128:(tt + 1) * 128], stk))
                            elif h[0] == "kv":
                                rr, rrk = rope_chunk(ps, pk, 2, tt)
                                transposes_to(rr, rrk, 2, lambda hh, tt=tt, stg=stg, stk=stk:
                                              (stg[:, hh, tt * 128:(tt + 1) * 128], stk))
                                vs, vsk = fst.next()
                                P.op("act", lambda e, vs=vs, ps=ps: e.activation(out=vs[:, 0:256], in_=ps[:, 256:512],
                                                                                 func=AF.Copy),
                                     reads=[pk], writes=[vsk])
                                P.dma(VA[bass.ds(t0, 128), :], vs[:, 0:256], reads=[vsk], writes=["VA"])
                            elif h[0] == "kiwi":
                                rr, rrk = rope_chunk(ps, pk, 1, tt)
                                transposes_to(rr, rrk, 1, lambda hh, tt=tt, stg=stg, stk=stk:
                                              (stg[:, hh, tt * 128:(tt + 1) * 128], stk))
                                wt, wtk = wir.next()
                                P.op("dve", lambda e, wt=wt, ps=ps: e.tensor_scalar(
                                    out=wt, in0=ps[:, 128:144], scalar1=float(16 ** -0.5 * 128 ** -0.5), scalar2=None,
                                    op0=ALU.mult), reads=[pk], writes=[wtk])
                                P.dma(WI[bass.ds(t0, 128), :], wt, reads=[wtk], writes=["WI"])
                            elif h[0] == "vc":
                                vs, vsk = fst.next()
                                P.op("act", lambda e, vs=vs, ps=ps: e.activation(out=vs, in_=ps, func=AF.Copy),
                                     reads=[pk], writes=[vsk])
                                P.dma(VC[bass.ds(t0, 128), h[1] * 512:(h[1] + 1) * 512], vs, reads=[vsk], writes=["VC"])
                        if h[0] == "rope":
                            for hh in range(4):
                                P.dma(h[1][h[2] + hh, :, bass.ds(g0, T)], stg[:, hh, :], reads=[stk], writes=[h[3]])
                        elif h[0] == "kv":
                            for hh in range(2):
                                P.dma(KAT[hh, :, bass.ds(g0, T)], stg[:, hh, :], reads=[stk], writes=["KAT"])
                        elif h[0] == "kiwi":
                            P.dma(KIT[:, bass.ds(g0, T)], stg[:, 0, :], reads=[stk], writes=["KIT"])
                    else:
                        for half in range(T // 512):
                            c0 = g0 + half * 512
                            if h[0] == "glu":
                                for sub in range(2):
                                    psa, pka = pmm.next()
                                    psg, pkg = pmm.next()
                                    for (ps_, pk_, so) in ((psa, pka, sub), (psg, pkg, sub + 2)):
                                        for k in range(16):
                                            P.op("pe", lambda e, k=k, ps_=ps_, so=so, half=half, wb=wb: e.matmul(
                                                ps_, lhsT=wb[:, k, so * 128:(so + 1) * 128],
                                                rhs=hT[:, k, half * 512:(half + 1) * 512],
                                                start=(k == 0), stop=(k == 15)), reads=["hT", wbk], writes=[pk_])
                                    sg, sgk = f32r.next()
                                    yy, yyk = f32r.next()
                                    P.op("act", lambda e, sg=sg, psg=psg: e.activation(out=sg, in_=psg, func=AF.Sigmoid),
                                         reads=[pkg], writes=[sgk])
                                    P.op("dve", lambda e, yy=yy, psa=psa, sg=sg: e.tensor_tensor(
                                        out=yy, in0=psa, in1=sg, op=ALU.mult), reads=[pka, sgk], writes=[yyk])
                                    P.dma(YT[h[1] + sub, :, bass.ds(c0, 512)], yy, reads=[yyk], writes=["YT"])
                            else:
                                func = {"silu": AF.Silu, "copy": AF.Copy, "sigm": AF.Sigmoid}[h[0]]
                                for sub in range(4):
                                    ps, pk = pmm.next()
                                    for k in range(16):
                                        P.op("pe", lambda e, k=k, ps=ps, sub=sub, half=half, wb=wb: e.matmul(
                                            ps, lhsT=wb[:, k, sub * 128:(sub + 1) * 128],
                                            rhs=hT[:, k, half * 512:(half + 1) * 512],
                                            start=(k == 0), stop=(k == 15)), reads=["hT", wbk], writes=[pk])
                                    fs, fsk = fst.next()
                                    P.op("act", lambda e, fs=fs, ps=ps, func=func: e.activation(out=fs, in_=ps, func=func),
                                         reads=[pk], writes=[fsk])
                                    P.dma(FMS[h[1] + sub, :, bass.ds(c0, 512)], fs, reads=[fsk], writes=["FMS"])
            P.end_phase()

    def phase_conv(l):
        with ExitStack() as st:
            ybr = Ring(st, nc, "yb", [128, 542], F32, 3)
            acc = sb(st, nc, "acc", [128, 8, 512], F32)
            sq = sb(st, nc, "sq", [128, 8, 512], F32)
            mean = sb(st, nc, "mean", [128, 512], F32)
            msq = sb(st, nc, "msq", [128, 512], F32)
            rstd = sb(st, nc, "rstd", [128, 512], F32)
            tr = Ring(st, nc, "tcv", [128, 512], F32, 2)
            ur = Ring(st, nc, "ucv", [128, 512], F32, 2)
            zr = Ring(st, nc, "zcv", [128, 512], BF16, 2)
            orr = Ring(st, nc, "ocv", [128, 512], BF16, 2)
            ps1, pk1 = banks[0], "psum0"
            ps2, pk2 = banks[1], "psum1"
            cwt = sb(st, nc, "cwt", [128, 8, CW], F32)
            cbt = sb(st, nc, "cbt", [128, 8], F32)
            clw = sb(st, nc, "clw", [128, 8], F32)
            clb = sb(st, nc, "clb", [128, 8], F32)
            P.dma(cwt, conv_w[l], writes=["c_cw"])
            P.dma(cbt, conv_b[l], writes=["c_cw"])
            P.dma(clw, cln_w[l], writes=["c_cw"])
            P.dma(clb, cln_b[l], writes=["c_cw"])
            for tc in range(NQC):
                t0 = tc * 512
                for cb in range(8):
                    yb, ybk = ybr.next()
                    if tc == 0:
                        P.op("pool", lambda e, yb=yb: e.memset(yb[:, 0:30], 0.0), writes=[ybk])
                        P.dma(yb[:, 30:542], YT[cb, :, 0:512], reads=["YT"], writes=[ybk])
                    else:
                        P.dma(yb, YT[cb, :, t0 - 30:t0 + 512], reads=["YT"], writes=[ybk])
                    ak = f"acc{cb}"
                    P.op("dve", lambda e, yb=yb, cb=cb: e.tensor_scalar(
                        out=acc[:, cb, :], in0=yb[:, 0:512], scalar1=cwt[:, cb, 0:1], scalar2=cbt[:, cb:cb + 1],
                        op0=ALU.mult, op1=ALU.add), reads=[ybk, "c_cw"], writes=[ak])
                    for k in range(1, CW):
                        P.op("dve", lambda e, yb=yb, cb=cb, k=k: e.scalar_tensor_tensor(
                            out=acc[:, cb, :], in0=yb[:, k:k + 512], scalar=cwt[:, cb, k:k + 1], in1=acc[:, cb, :],
                            op0=ALU.mult, op1=ALU.add), reads=[ybk, ak], writes=[ak])
                    P.op("act", lambda e, cb=cb: e.activation(out=sq[:, cb, :], in_=acc[:, cb, :], func=AF.Square),
                         reads=[ak], writes=[f"sq{cb}"])
                for cb in range(8):
                    P.op("pe", lambda e, cb=cb: e.matmul(ps1, lhsT=onesf, rhs=acc[:, cb, :], start=(cb == 0),
                                                         stop=(cb == 7)), reads=[f"acc{cb}", "c_onesf"], writes=[pk1])
                for cb in range(8):
                    P.op("pe", lambda e, cb=cb: e.matmul(ps2, lhsT=onesf, rhs=sq[:, cb, :], start=(cb == 0),
                                                         stop=(cb == 7)), reads=[f"sq{cb}", "c_onesf"], writes=[pk2])
                P.op("dve", lambda e: e.tensor_scalar(out=mean, in0=ps1, scalar1=1.0 / 1024, scalar2=None, op0=ALU.mult),
                     reads=[pk1], writes=["mean"])
                P.op("pool", lambda e: e.tensor_tensor(out=msq, in0=mean, in1=mean, op=ALU.mult),
                     reads=["mean"], writes=["msq"])
                P.op("dve", lambda e: e.scalar_tensor_tensor(out=rstd, in0=ps2, scalar=1.0 / 1024, in1=msq,
                                                             op0=ALU.mult, op1=ALU.subtract),
                     reads=[pk2, "msq"], writes=["rstd"])
                P.op("dve", lambda e: e.tensor_scalar(out=rstd, in0=rstd, scalar1=EPS, scalar2=None, op0=ALU.add),
                     reads=["rstd"], writes=["rstd"])
                P.op("act", lambda e: e.activation(out=rstd, in_=rstd, func=AF.Sqrt), reads=["rstd"], writes=["rstd"])
                P.op("dve", lambda e: e.reciprocal(out=rstd, in_=rstd), reads=["rstd"], writes=["rstd"])
                for cb in range(8):
                    tt_, tk = tr.next()
                    uu, uk = ur.next()
                    zz, zk = zr.next()
                    oo, ok = orr.next()
                    P.dma(zz, FMS[FM_ZB + cb, :, t0:t0 + 512], reads=["FMS"], writes=[zk])
                    P.op("dve", lambda e, tt_=tt_, cb=cb: e.tensor_tensor(out=tt_, in0=acc[:, cb, :], in1=mean,
                                                                         op=ALU.subtract),
                         reads=[f"acc{cb}", "mean"], writes=[tk])
                    P.op("pool", lambda e, tt_=tt_: e.tensor_tensor(out=tt_, in0=tt_, in1=rstd, op=ALU.mult),
                         reads=[tk, "rstd"], writes=[tk])
                    P.op("act", lambda e, tt_=tt_, uu=uu, cb=cb: e.activation(
                        out=uu, in_=tt_, func=AF.Silu, scale=clw[:, cb:cb + 1], bias=clb[:, cb:cb + 1]),
                        reads=[tk, "c_cw"], writes=[uk])
                    P.op("pool", lambda e, uu=uu, zz=zz, oo=oo: e.tensor_tensor(out=oo, in0=uu, in1=zz, op=ALU.mult),
                         reads=[uk, zk], writes=[ok])
                    P.dma(OGT[8 + cb, :, t0:t0 + 512], oo, reads=[ok], writes=["OGT"])
            P.end_phase()

    def phase_sb(l):
        sc = float(HD ** -0.5)
        with ExitStack() as st:
            vsb = sb(st, nc, "vsb", [128, NT, 128], BF16)
            ktr = Ring(st, nc, "kts", [128, S], BF16, 2)
            qr = Ring(st, nc, "qsb", [128, 512], BF16, 2)
            zgr = Ring(st, nc, "zgs", [128, 512], BF16, 2)
            er = Ring(st, nc, "esb", [128, 512], F32, 2)
            spr = Ring(st, nc, "spb", [128, 512], F32, 3)
            lkr = Ring(st, nc, "lkb", [128, 512], BF16, 3)
            dr = Ring(st, nc, "dsb", [128, 512], F32, 2)
            ar = Ring(st, nc, "argb", [128, 512], F32, 2)
            atr = Ring(st, nc, "atb", [128, 512], BF16, 3)
            cr = Ring(st, nc, "car", [128, 512], F32, 2)
            ogr = Ring(st, nc, "ogs", [128, 512], BF16, 2)
            pz = PRing(banks, [0, 1])
            pa = PRing(banks, [2, 3])
            pt = PRing(banks, [4, 5])
            po = PRing(banks, [6, 7])
            m01 = sb(st, nc, "m01", [128, 4, 512], BF16)
            mbias = sb(st, nc, "mbias", [128, 4, 512], F32)
            for j in range(4):
                P.op("pool", lambda e, j=j: e.affine_select(out=m01[:, j, :], in_=ones512, pattern=[[1, 512]],
                                                            compare_op=ALU.is_ge, fill=0.0, base=-128 * j - 1,
                                                            channel_multiplier=-1),
                     reads=["c_ones512"], writes=["c_m01"])
                P.op("pool", lambda e, j=j: e.affine_select(out=mbias[:, j, :], in_=zer512f, pattern=[[1, 512]],
                                                            compare_op=ALU.is_ge, fill=NEG, base=-128 * j - 1,
                                                            channel_multiplier=-1),
                     reads=["c_zer512f"], writes=["c_mbias"])
            for h in range(8):
                P.dma(vsb, VC[:, bass.ds(h * 128, 128)].rearrange("(t p) c -> p t c", p=128), reads=["VC"], writes=["vsb"])
                kt, ktk = ktr.next()
                P.dma(kt, FMS[FM_KC + h], reads=["FMS"], writes=[ktk])
                for qc in range(NQC):
                    q0 = qc * 512
                    qs, qk = qr.next()
                    zg, zgk = zgr.next()
                    P.dma(qs, FMS[FM_QC + h, :, q0:q0 + 512], reads=["FMS"], writes=[qk])
                    P.dma(zg, FMS[FM_ZC + h, :, q0:q0 + 512], reads=["FMS"], writes=[zgk])
                    car, cak = cr.next()
                    P.op("pool", lambda e, car=car: e.memset(car, 0.0), writes=[cak])
                    ot, otk = po.next()
                    nkb = 4 * qc + 4
                    for i, kb in enumerate(range(nkb - 1, -1, -1)):
                        j = kb - 4 * qc
                        zt, ztk = pz.next()
                        P.op("pe", lambda e, zt=zt, kt=kt, kb=kb, qs=qs: e.matmul(
                            zt, lhsT=kt[:, kb * 128:(kb + 1) * 128], rhs=qs, start=True, stop=True),
                            reads=[ktk, qk], writes=[ztk])
                        ee, ek = er.next()
                        sp_, spk = spr.next()
                        lk, lkk = lkr.next()
                        P.op("act", lambda e, ee=ee, zt=zt: e.activation(out=ee, in_=zt, func=AF.Exp, scale=-sc),
                             reads=[ztk], writes=[ek])
                        P.op("act", lambda e, ee=ee, sp_=sp_: e.activation(out=sp_, in_=ee, func=AF.Ln, bias=1.0),
                             reads=[ek], writes=[spk])
                        P.op("dve", lambda e, lk=lk, zt=zt, sp_=sp_: e.scalar_tensor_tensor(
                            out=lk, in0=zt, scalar=-sc, in1=sp_, op0=ALU.mult, op1=ALU.subtract),
                            reads=[ztk, spk], writes=[lkk])
                        if j >= 0:
                            P.op("pool", lambda e, lk=lk, j=j: e.tensor_tensor(out=lk, in0=lk, in1=m01[:, j, :],
                                                                                op=ALU.mult),
                                 reads=[lkk, "c_m01"], writes=[lkk])
                        af, afk = pa.next()
                        tt_, ttk = pt.next()
                        P.op("pe", lambda e, af=af, lk=lk: e.matmul(af, lhsT=utri, rhs=lk, start=True, stop=True),
                             reads=["c_utri", lkk], writes=[afk])
                        P.op("pe", lambda e, tt_=tt_, lk=lk: e.matmul(tt_, lhsT=onesb, rhs=lk, start=True, stop=True),
                             reads=["c_onesb", lkk], writes=[ttk])
                        dd, dk = dr.next()
                        P.op("pool", lambda e, dd=dd, car=car, sp_=sp_: e.tensor_tensor(out=dd, in0=car, in1=sp_,
                                                                                        op=ALU.subtract),
                             reads=[cak, spk], writes=[dk])
                        if j >= 0:
                            P.op("pool", lambda e, dd=dd, j=j: e.tensor_tensor(out=dd, in0=dd, in1=mbias[:, j, :],
                                                                                op=ALU.add),
                                 reads=[dk, "c_mbias"], writes=[dk])
                        ag, agk = ar.next()
                        P.op("dve", lambda e, ag=ag, af=af, dd=dd: e.tensor_tensor(out=ag, in0=af, in1=dd, op=ALU.add),
                             reads=[afk, dk], writes=[agk])
                        at, atk = atr.next()
                        P.op("act", lambda e, at=at, ag=ag: e.activation(out=at, in_=ag, func=AF.Exp),
                             reads=[agk], writes=[atk])
                        P.op("dve", lambda e, car=car, tt_=tt_: e.tensor_tensor(out=car, in0=tt_, in1=car, op=ALU.add),
                             reads=[ttk, cak], writes=[cak])
                        P.op("pe", lambda e, ot=ot, kb=kb, at=at, i=i, nkb=nkb, h=h: e.matmul(
                            ot, lhsT=vsb[:, kb, :], rhs=at, start=(i == 0), stop=(i == nkb - 1)),
                            reads=["vsb", atk], writes=[otk])
                    og, ogk = ogr.next()
                    P.op("dve", lambda e, og=og, ot=ot, zg=zg: e.tensor_tensor(out=og, in0=ot, in1=zg, op=ALU.mult),
                         reads=[otk, zgk], writes=[ogk])
                    P.dma(OGT[16 + h, :, q0:q0 + 512], og, reads=[ogk], writes=["OGT"])
            P.end_phase()

    def phase_dsa(l):
        sc = float(HD ** -0.5)
        with ExitStack() as st:
            kat = sb(st, nc, "kat_s", [128, 2, S], BF16)
            kit = sb(st, nc, "kit_s", [128, S], BF16)
            vas = sb(st, nc, "vas", [128, NT, 256], BF16)
            qir = Ring(st, nc, "qis", [128, 16, 128], BF16, 2)
            qar = Ring(st, nc, "qas", [128, 8, 128], BF16, 2)
            wr = Ring(st, nc, "wis", [128, 16], F32, 2)
            zar = Ring(st, nc, "zas", [128, 8, 128], BF16, 2)
            Ir = Ring(st, nc, "Isc", [128, S], F32, 2)
            Wk = sb(st, nc, "Wk", [128, S], F32)
            rr = Ring(st, nc, "rel", [128, 512], F32, 3)
            m8r = Ring(st, nc, "m8", [128, 8], F32, 2)
            thr_ = Ring(st, nc, "thr", [128, 1], F32, 2)
            mkr = Ring(st, nc, "msk", [128, S], BF16, 2)
            mtr = Ring(st, nc, "mkT", [128, NT, 128], BF16, 2)
            er = Ring(st, nc, "edsa", [128, 512], BF16, 3)
            pr = Ring(st, nc, "pdsa", [128, 512], BF16, 3)
            rdr = Ring(st, nc, "rden", [128, 512], F32, 2)
            o1r = Ring(st, nc, "o1", [128, 512], F32, 2)
            ogr = Ring(st, nc, "ogd", [128, 512], BF16, 2)
            pI = PRing(banks, [0, 1])
            pT_ = PRing(banks, [2])
            pL = PRing(banks, [3, 4])
            pO = PRing(banks, [5, 6])
            pD = PRing(banks, [7])
            P.dma(kat, KAT.rearrange("c p s -> p c s"), reads=["KAT"], writes=["kat_s"])
            P.dma(kit, KIT, reads=["KIT"], writes=["kit_s"])
            P.dma(vas, VA.rearrange("(t p) c -> p t c", p=128), reads=["VA"], writes=["vas"])
            for qt in range(NT):
                q0 = qt * 128
                Lk = q0 + 128
                nkb = qt + 1
                qi, qik = qir.next()
                qa, qak = qar.next()
                wt, wtk = wr.next()
                za, zak = zar.next()
                P.dma(qi, QIT[:, :, q0:q0 + 128].rearrange("h p q -> p h q"), reads=["QIT"], writes=[qik])
                P.dma(qa, QAT[:, :, q0:q0 + 128].rearrange("h p q -> p h q"), reads=["QAT"], writes=[qak])
                P.dma(wt, WI[q0:q0 + 128, :], reads=["WI"], writes=[wtk])
                P.dma(za, FMS[FM_ZA:FM_ZA + 8, :, q0:q0 + 128].rearrange("h p q -> p h q"), reads=["FMS"], writes=[zak])
                I, Ik = Ir.next()
                nch = (Lk + 511) // 512
                for c in range(nch):
                    c0 = c * 512
                    cn = min(512, Lk - c0)
                    for hh in range(16):
                        ps, pk = pI.next()
                        P.op("pe", lambda e, ps=ps, qi=qi, hh=hh, c0=c0, cn=cn: e.matmul(
                            ps[:, 0:cn], lhsT=qi[:, hh, :], rhs=kit[:, c0:c0 + cn], start=True, stop=True),
                            reads=[qik, "kit_s"], writes=[pk])
                        rl, rlk = rr.next()
                        P.op("act", lambda e, rl=rl, ps=ps, cn=cn: e.activation(out=rl[:, 0:cn], in_=ps[:, 0:cn],
                                                                               func=AF.Relu),
                             reads=[pk], writes=[rlk])
                        if hh == 0:
                            P.op("dve", lambda e, I=I, rl=rl, wt=wt, c0=c0, cn=cn: e.tensor_scalar(
                                out=I[:, c0:c0 + cn], in0=rl[:, 0:cn], scalar1=wt[:, 0:1], scalar2=None, op0=ALU.mult),
                                reads=[rlk, wtk], writes=[Ik])
                        else:
                            P.op("dve", lambda e, I=I, rl=rl, wt=wt, c0=c0, cn=cn, hh=hh: e.scalar_tensor_tensor(
                                out=I[:, c0:c0 + cn], in0=rl[:, 0:cn], scalar=wt[:, hh:hh + 1], in1=I[:, c0:c0 + cn],
                                op0=ALU.mult, op1=ALU.add), reads=[rlk, wtk, Ik], writes=[Ik])
                P.op("pool", lambda e, I=I, q0=q0: e.affine_select(
                    out=I[:, q0:q0 + 128], in_=I[:, q0:q0 + 128], pattern=[[-1, 128]], compare_op=ALU.is_ge,
                    fill=NEG, base=0, channel_multiplier=1), reads=[Ik], writes=[Ik])
                th, thk = thr_.next()
                if Lk <= TOPK:
                    P.op("pool", lambda e, th=th: e.memset(th, -1.0e29), writes=[thk])
                else:
                    m8 = None
                    for r in range(TOPK // 8):
                        m8, m8k = m8r.next()
                        src = I if r == 0 else Wk
                        srck = Ik if r == 0 else "Wk"
                        P.op("dve", lambda e, m8=m8, src=src, Lk=Lk: e.max(out=m8, in_=src[:, 0:Lk]),
                             reads=[srck], writes=[m8k])
                        if r < TOPK // 8 - 1:
                            P.op("dve", lambda e, m8=m8, src=src, Lk=Lk: e.match_replace(
                                out=Wk[:, 0:Lk], in_to_replace=m8, in_values=src[:, 0:Lk], imm_value=NEG),
                                reads=[srck, m8k], writes=["Wk"])
                    P.op("dve", lambda e, th=th, m8=m8: e.tensor_copy(out=th, in_=m8[:, 7:8]),
                         reads=[m8k], writes=[thk])
                mk, mkk = mkr.next()
                P.op("dve", lambda e, mk=mk, I=I, th=th, Lk=Lk: e.tensor_scalar(
                    out=mk[:, 0:Lk], in0=I[:, 0:Lk], scalar1=th, scalar2=None, op0=ALU.is_ge),
                    reads=[Ik, thk], writes=[mkk])
                mt, mtk = mtr.next()
                kb = 0
                while kb < nkb:
                    nb = min(8, nkb - kb)
                    pb, pk = pT_.next()
                    pbb = pb.bitcast(BF16)
                    for jj in range(nb):
                        P.op("pe", lambda e, pbb=pbb, jj=jj, mk=mk, kb=kb: e.transpose(
                            out=pbb[:, jj * 128:(jj + 1) * 128], in_=mk[:, (kb + jj) * 128:(kb + jj + 1) * 128],
                            identity=identb), reads=[mkk, "c_ident"], writes=[pk])
                    P.op("act", lambda e, pbb=pbb, mt=mt, kb=kb, nb=nb: e.activation(
                        out=mt[:, kb:kb + nb, :], in_=pbb[:, 0:nb * 128].rearrange("p (k q) -> p k q", k=nb),
                        func=AF.Copy), reads=[pk], writes=[mtk])
                    kb += nb
                for c in range(2):
                    po_, pok = pO.next()
                    pd_, pdk = pD.next()
                    for kb in range(nkb):
                        pl, plk = pL.next()
                        P.op("pe", lambda e, pl=pl, c=c, kb=kb, qa=qa: e.matmul(
                            pl, lhsT=kat[:, c, kb * 128:(kb + 1) * 128],
                            rhs=qa[:, 4 * c:4 * c + 4, :].rearrange("p h q -> p (h q)"), start=True, stop=True),
                            reads=["kat_s", qak], writes=[plk])
                        ee, ek = er.next()
                        P.op("act", lambda e, ee=ee, pl=pl: e.activation(out=ee, in_=pl, func=AF.Exp, scale=sc),
                             reads=[plk], writes=[ek])
                        pp, ppk = pr.next()
                        P.op("pool", lambda e, pp=pp, ee=ee, mt=mt, kb=kb: e.tensor_tensor(
                            out=pp.rearrange("p (h q) -> p h q", h=4), in0=ee.rearrange("p (h q) -> p h q", h=4),
                            in1=mt[:, kb, :].unsqueeze(1).to_broadcast([128, 4, 128]), op=ALU.mult),
                            reads=[ek, mtk], writes=[ppk])
                        P.op("pe", lambda e, po_=po_, kb=kb, c=c, pp=pp, nkb=nkb: e.matmul(
                            po_, lhsT=vas[:, kb, c * 128:(c + 1) * 128], rhs=pp, start=(kb == 0), stop=(kb == nkb - 1)),
                            reads=["vas", ppk], writes=[pok])
                        P.op("pe", lambda e, pd_=pd_, kb=kb, pp=pp, nkb=nkb: e.matmul(
                            pd_, lhsT=onesb, rhs=pp, start=(kb == 0), stop=(kb == nkb - 1)),
                            reads=["c_onesb", ppk], writes=[pdk])
                    rd, rdk = rdr.next()
                    o1, o1k = o1r.next()
                    og, ogk = ogr.next()
                    P.op("dve", lambda e, rd=rd, pd_=pd_: e.reciprocal(out=rd, in_=pd_), reads=[pdk], writes=[rdk])
                    P.op("dve", lambda e, o1=o1, po_=po_, rd=rd: e.tensor_tensor(out=o1, in0=po_, in1=rd, op=ALU.mult),
                         reads=[pok, rdk], writes=[o1k])
                    P.op("pool", lambda e, og=og, o1=o1, za=za, c=c: e.tensor_tensor(
                        out=og, in0=o1, in1=za[:, 4 * c:4 * c + 4, :].rearrange("p h q -> p (h q)"), op=ALU.mult),
                        reads=[o1k, zak], writes=[ogk])
                    P.dma(OGT[4 * c:4 * c + 4, :, q0:q0 + 128].rearrange("h p q -> p h q"),
                          og.rearrange("p (h q) -> p h q", h=4), reads=[ogk], writes=["OGT"])
            P.end_phase()

    def phase_C1(l):
        with ExitStack() as st:
            wst = Ring(st, nc, "wst", [128, 8, 256], F32, 2)
            wo = [sb(st, nc, f"wo{i}", [128, 8, D], BF16) for i in range(3)]
            ogr = Ring(st, nc, "ogc", [128, 24, 512], BF16, 1)
            gr = Ring(st, nc, "gc", [128, 512], BF16, 6)
            t3 = Ring(st, nc, "t3", [128, 512], F32, 6)
            mr = Ring(st, nc, "mc", [128, 512], BF16, 2)
            pY = PRing(banks, [0, 1, 2, 3, 4, 5])
            for i, wsrc in enumerate((w_ao, w_bo, w_co)):
                for cc in range(8):
                    ws, wsk = wst.next()
                    src = wsrc[l, :, cc * 256:(cc + 1) * 256].rearrange("(k p) n -> p k n", p=128)
                    P.dma(ws, src, writes=[wsk])
                    P.op("pool", lambda e, ws=ws, i=i, cc=cc: e.tensor_copy(out=wo[i][:, :, cc * 256:(cc + 1) * 256],
                                                                            in_=ws), reads=[wsk], writes=[f"wo{i}"])

            def gload(tc, dc):
                res = []
                for i in range(3):
                    gg, ggk = gr.next()
                    P.dma(gg, FMS[FM_G + i * 16 + dc, :, tc * 512:(tc + 1) * 512], reads=["FMS"], writes=[ggk])
                    res.append((gg, ggk))
                return res

            steps = [(tc, dc) for tc in range(NQC) for dc in range(16)]
            gnext = gload(*steps[0])
            ogt = ogk = None
            for si, (tc, dc) in enumerate(steps):
                t0 = tc * 512
                if dc == 0:
                    ogt, ogk = ogr.next()
                    P.dma(ogt, OGT[:, :, t0:t0 + 512].rearrange("c p t -> p c t"), reads=["OGT"], writes=[ogk])
                gcur = gnext
                if si + 1 < len(steps):
                    gnext = gload(*steps[si + 1])
                ts_ = []
                for i in range(3):
                    py, pyk = pY.next()
                    for k in range(8):
                        P.op("pe", lambda e, py=py, i=i, k=k, dc=dc, ogt=ogt: e.matmul(
                            py, lhsT=wo[i][:, k, dc * 128:(dc + 1) * 128], rhs=ogt[:, i * 8 + k, :],
                            start=(k == 0), stop=(k == 7)), reads=[f"wo{i}", ogk], writes=[pyk])
                    gg, ggk = gcur[i]
                    tq, tqk = t3.next()
                    P.op("dve", lambda e, tq=tq, py=py, gg=gg: e.tensor_tensor(out=tq, in0=py, in1=gg, op=ALU.mult),
                         reads=[pyk, ggk], writes=[tqk])
                    ts_.append((tq, tqk))
                mm, mmk = mr.next()
                P.op("pool", lambda e, ts_=ts_: e.tensor_tensor(out=ts_[0][0], in0=ts_[0][0], in1=ts_[1][0], op=ALU.add),
                     reads=[ts_[0][1], ts_[1][1]], writes=[ts_[0][1]])
                P.op("pool", lambda e, ts_=ts_, mm=mm: e.tensor_tensor(out=mm, in0=ts_[0][0], in1=ts_[2][0], op=ALU.add),
                     reads=[ts_[0][1], ts_[2][1]], writes=[mmk])
                P.dma(MGT[dc, :, t0:t0 + 512], mm, reads=[mmk], writes=["MGT"])
            P.end_phase()

    def phase_C2(l, xcur, xkey):
        with ExitStack() as st:
            wst = Ring(st, nc, "wst", [128, 8, 512], F32, 2)
            wout = sb(st, nc, "wout", [128, 16, D], BF16)
            mgr = Ring(st, nc, "mgc", [128, 16, 128], BF16, 3)
            xr = Ring(st, nc, "xc2", [128, D], F32, 3)
            xnr = Ring(st, nc, "xn2", [128, D], F32, 2)
            pX = PRing(banks, [0, 1, 2, 3])
            for cc in range(4):
                for kh in range(2):
                    ws, wsk = wst.next()
                    src = w_o[l, kh * 1024:(kh + 1) * 1024, cc * 512:(cc + 1) * 512].rearrange("(k p) n -> p k n", p=128)
                    P.dma(ws, src, writes=[wsk])
                    P.op("pool", lambda e, ws=ws, kh=kh, cc=cc: e.tensor_copy(
                        out=wout[:, kh * 8:(kh + 1) * 8, cc * 512:(cc + 1) * 512], in_=ws), reads=[wsk], writes=["wout"])

            def c2load(tt):
                mg, mgk = mgr.next()
                xt, xk = xr.next()
                P.dma(mg, MGT[:, :, tt * 128:(tt + 1) * 128].rearrange("c p t -> p c t"), reads=["MGT"], writes=[mgk])
                P.dma(xt, xcur[tt * 128:(tt + 1) * 128, :], reads=[f"{xkey}{tt}"], writes=[xk])
                return mg, mgk, xt, xk

            ldn = c2load(0)
            for tt in range(NT):
                t0 = tt * 128
                mg, mgk, xt, xk = ldn
                if tt + 1 < NT:
                    ldn = c2load(tt + 1)
                xn, xnk = xnr.next()
                for cc in range(4):
                    px, pxk = pX.next()
                    for k in range(16):
                        P.op("pe", lambda e, px=px, mg=mg, k=k, cc=cc: e.matmul(
                            px, lhsT=mg[:, k, :], rhs=wout[:, k, cc * 512:(cc + 1) * 512], start=(k == 0), stop=(k == 15)),
                            reads=[mgk, "wout"], writes=[pxk])
                    P.op("dve", lambda e, xn=xn, px=px, xt=xt, cc=cc: e.tensor_tensor(
                        out=xn[:, cc * 512:(cc + 1) * 512], in0=px, in1=xt[:, cc * 512:(cc + 1) * 512], op=ALU.add),
                        reads=[pxk, xk], writes=[xnk])
                P.dma(xcur[t0:t0 + 128, :], xn, reads=[xnk], writes=[f"{xkey}{tt}"])
            P.end_phase()

    def phase_final(xcur):
        with ExitStack() as st:
            xr = Ring(st, nc, "xf", [128, D], F32, 3)
            ssr = Ring(st, nc, "ssf", [128, 1], F32, 2)
            jnk = sb(st, nc, "jnkf", [128, D], BF16)
            nwt = sb(st, nc, "nwtf", [128, D], F32)
            P.dma(nwt, fnw.partition_broadcast(128), writes=["nwtf"])
            for tt in range(NT):
                t0 = tt * 128
                xn, xnk = xr.next()
                ss, ssk = ssr.next()
                P.dma(xn, xcur[t0:t0 + 128, :], writes=[xnk])
                P.op("act", lambda e, xn=xn, ss=ss: e.activation(out=jnk, in_=xn, func=AF.Square, accum_out=ss),
                     reads=[xnk], writes=["jnkf", ssk])
                P.op("dve", lambda e, ss=ss: e.tensor_scalar(out=ss, in0=ss, scalar1=1.0 / D, scalar2=EPS,
                                                             op0=ALU.mult, op1=ALU.add), reads=[ssk], writes=[ssk])
                P.op("act", lambda e, ss=ss: e.activation(out=ss, in_=ss, func=AF.Sqrt), reads=[ssk], writes=[ssk])
                P.op("dve", lambda e, ss=ss: e.reciprocal(out=ss, in_=ss), reads=[ssk], writes=[ssk])
                P.op("dve", lambda e, xn=xn, ss=ss: e.scalar_tensor_tensor(
                    out=xn, in0=xn, scalar=ss, in1=nwt, op0=ALU.mult, op1=ALU.mult),
                    reads=[xnk, ssk, "nwtf"], writes=[xnk])
                P.dma(out[t0:t0 + 128, :], xn, reads=[xnk], writes=["out"])
            P.end_phase()

    src_w = dict(norm_w=norm_w, w_in=w_in, conv_w=conv_w, conv_b=conv_b, cln_w=cln_w, cln_b=cln_b,
                 w_ao=w_ao, w_bo=w_bo, w_co=w_co, w_o=w_o)
    norm_w = dscr("norm_w_s", [1, D], F32)
    w_in = dscr("w_in_s", [1, D, IN_WIDTH], F32)
    conv_w = dscr("conv_w_s", [1, 128, 8, CW], F32)
    conv_b = dscr("conv_b_s", [1, 128, 8], F32)
    cln_w = dscr("cln_w_s", [1, 128, 8], F32)
    cln_b = dscr("cln_b_s", [1, 128, 8], F32)
    w_ao = dscr("w_ao_s", [1, 1024, D], F32)
    w_bo = dscr("w_bo_s", [1, 1024, D], F32)
    w_co = dscr("w_co_s", [1, 1024, D], F32)
    w_o = dscr("w_o_s", [1, D, D], F32)
    lctx = nc.Fori(0, L)
    lv = lctx.__enter__()
    P.loop_top()
    for r0 in range(0, D, 512):
        P.dma(w_in[0, r0:r0 + 512, :], src_w["w_in"][lv, r0:r0 + 512, :], writes=["w_in_s"])
    P.dma(norm_w[0], src_w["norm_w"][lv], writes=["misc_s"])
    P.dma(conv_w[0], src_w["conv_w"][lv], writes=["misc_s"])
    P.dma(conv_b[0], src_w["conv_b"][lv], writes=["misc_s"])
    P.dma(cln_w[0], src_w["cln_w"][lv], writes=["misc_s"])
    P.dma(cln_b[0], src_w["cln_b"][lv], writes=["misc_s"])
    P.dma(w_ao[0], src_w["w_ao"][lv], writes=["misc_s"])
    P.dma(w_bo[0], src_w["w_bo"][lv], writes=["misc_s"])
    P.dma(w_co[0], src_w["w_co"][lv], writes=["misc_s"])
    for r0 in range(0, D, 1024):
        P.dma(w_o[0, r0:r0 + 1024, :], src_w["w_o"][lv, r0:r0 + 1024, :], writes=["misc_s"])
    P.end_phase()
    l = 0
    phase_A(l, XS[0], "xs0")
    phase_conv(l)
    phase_sb(l)
    phase_dsa(l)
    phase_C1(l)
    phase_C2(l, XS[0], "xs0_")
    lctx.__exit__(None, None, None)
    phase_final(XS[0])
    gst.close()
    return nc, P


def rope_table(S):
    inv = (1.0 / (np.float32(10000.0) ** (np.arange(0, HD, 2, dtype=np.float32) / np.float32(HD)))).astype(np.float32)
    ang = np.arange(S, dtype=np.float32)[:, None] * inv[None, :]
    c = np.cos(ang).astype(np.float32)
    s = np.sin(ang).astype(np.float32)
    return np.ascontiguousarray(np.concatenate([c, s, -s], axis=1))


def layout_inputs(inp, b, S, L):
    f = lambda a: np.ascontiguousarray(np.asarray(a, dtype=np.float32))
    cw = np.asarray(inp["conv_w"], np.float32)[:L, :, 0, :]
    cw = cw.transpose(0, 2, 1).reshape(L, 8, 128, CW).transpose(0, 2, 1, 3)
    pv = lambda a: np.ascontiguousarray(np.asarray(a, np.float32)[:L].reshape(L, 8, 128).transpose(0, 2, 1))
    return {
        "x": f(np.asarray(inp["x"])[b, :S]),
        "norm_w": f(np.asarray(inp["norm_w"])[:L]),
        "w_in": f(np.asarray(inp["w_in"])[:L]),
        "conv_w": np.ascontiguousarray(cw),
        "conv_b": pv(inp["conv_b"]),
        "conv_ln_w": pv(inp["conv_ln_w"]),
        "conv_ln_b": pv(inp["conv_ln_b"]),
        "w_a_out": f(np.asarray(inp["w_a_out"])[:L]),
        "w_b_out": f(np.asarray(inp["w_b_out"])[:L]),
        "w_c_out": f(np.asarray(inp["w_c_out"])[:L]),
        "w_out": f(np.asarray(inp["w_out"])[:L]),
        "final_norm_w": f(inp["final_norm_w"]),
        "rope": rope_table(S),
    }


def kernel(**inputs):
    x = np.asarray(inputs["x"])
    B, S, _ = x.shape
    L = np.asarray(inputs["norm_w"]).shape[0]
    nc, _ = build(S, L)
    shared = layout_inputs(inputs, 0, S, L)
    in_maps = []
    for b in range(B):
        m = dict(shared)
        m["x"] = np.ascontiguousarray(x[b], dtype=np.float32)
        in_maps.append(m)
    res = run_bass_kernel_spmd(nc, in_maps, core_ids=list(range(B)))
    return np.stack([np.asarray(res.results[b]["out"], dtype=np.float32) for b in range(B)], axis=0)
```
